# Optimizing a Trainium2 kernel written in Bass

```python
import jax, jax.numpy as jnp
from jax import lax
import numpy as np

D_MODEL = 1024
BATCH = 32
SEQ = 2048
DEPTH = 1

MEM_LEN = 256
CONV_WIDTH = 31
CONV_DIM = 1024
SGU_DIM = 1024
SGU_GROUPS = 8
SGU_CHUNK = 128
XATTN_HEADS = 4
XATTN_HEAD_DIM = D_MODEL // XATTN_HEADS
FFN_HIDDEN = ((-(-8 * D_MODEL // 3) + 255) // 256) * 256
IN_COLS = 2 * CONV_DIM + 2 * SGU_DIM + 2 * D_MODEL
RMS_EPS = 1e-6
LN_EPS = 1e-5

kernel_name = "hybrid_conv_sgu_gated_block"


def rmsnorm(x, g):
    xf = x.astype(jnp.float32)
    y = xf * lax.rsqrt(jnp.mean(xf * xf, axis=-1, keepdims=True) + RMS_EPS)
    return (y * g.astype(jnp.float32)).astype(x.dtype)


def layernorm(x, g, b):
    xf = x.astype(jnp.float32)
    mu = jnp.mean(xf, axis=-1, keepdims=True)
    var = jnp.mean(jnp.square(xf - mu), axis=-1, keepdims=True)
    y = (xf - mu) * lax.rsqrt(var + LN_EPS) * g.astype(jnp.float32) + b.astype(jnp.float32)
    return y.astype(x.dtype)


def causal_depthwise_conv(x, w, b):
    y = lax.conv_general_dilated(
        x, w[:, None, :], window_strides=(1,), padding=[(CONV_WIDTH - 1, 0)],
        dimension_numbers=("NWC", "WIO", "NWC"), feature_group_count=x.shape[-1])
    return y + b


def chunked_spatial_gating(u, v, w_s, b_s):
    B, S, C = v.shape
    n_chunks = S // SGU_CHUNK
    gd = C // SGU_GROUPS
    mask = jnp.tril(jnp.ones((SGU_CHUNK, SGU_CHUNK), dtype=bool))
    w = jnp.where(mask[None], w_s, jnp.zeros_like(w_s))
    vc = v.reshape(B, n_chunks, SGU_CHUNK, SGU_GROUPS, gd)
    z = jnp.einsum('gts,bnsgc->bntgc', w, vc) + jnp.transpose(b_s)[None, None, :, :, None]
    return u * z.reshape(B, S, C)


def mixer_block(h, w_in, b_gate, conv_w, conv_b, conv_ln_g, conv_ln_b, w_conv_out,
                sgu_ln_g, sgu_ln_b, sgu_w, sgu_b, w_sgu_out, w_mix_out):
    p = jnp.einsum('bsd,de->bse', h, w_in)
    a_val, a_gate, b_u, b_v, g_a, g_b = jnp.split(
        p, np.cumsum([CONV_DIM, CONV_DIM, SGU_DIM, SGU_DIM, D_MODEL]).tolist(), axis=-1)
    a = a_val * jax.nn.sigmoid(a_gate)
    a = causal_depthwise_conv(a, conv_w, conv_b)
    a = jax.nn.silu(layernorm(a, conv_ln_g, conv_ln_b))
    y_a = jnp.einsum('bsc,cd->bsd', a, w_conv_out)
    u = jax.nn.gelu(b_u)
    v = layernorm(jax.nn.gelu(b_v), sgu_ln_g, sgu_ln_b)
    y_b = jnp.einsum('bsc,cd->bsd', chunked_spatial_gating(u, v, sgu_w, sgu_b), w_sgu_out)
    merged = jax.nn.sigmoid(g_a + b_gate[0]) * y_a + jax.nn.sigmoid(g_b + b_gate[1]) * y_b
    return jnp.einsum('bsd,de->bse', merged, w_mix_out)


def memory_cross_attention(h, mem_n, w_q, w_kv, w_xo):
    B, S, _ = h.shape
    M = mem_n.shape[1]
    q = jnp.einsum('bsd,de->bse', h, w_q).reshape(B, S, XATTN_HEADS, XATTN_HEAD_DIM)
    kv = jnp.einsum('bmd,de->bme', mem_n, w_kv)
    k, v = jnp.split(kv, 2, axis=-1)
    k = k.reshape(B, M, XATTN_HEADS, XATTN_HEAD_DIM)
    v = v.reshape(B, M, XATTN_HEADS, XATTN_HEAD_DIM)
    s = jnp.einsum('bshd,bmhd->bhsm', q, k).astype(jnp.float32) * (XATTN_HEAD_DIM ** -0.5)
    pr = jax.nn.softmax(s, axis=-1).astype(v.dtype)
    o = jnp.einsum('bhsm,bmhd->bshd', pr, v).reshape(B, S, D_MODEL)
    return jnp.einsum('bsd,de->bse', o, w_xo)


def swiglu_ffn(h, w_gu, w_down):
    gu = jnp.einsum('bsd,df->bsf', h, w_gu)
    gt, up = jnp.split(gu, 2, axis=-1)
    return jnp.einsum('bsf,fd->bsd', jax.nn.silu(gt) * up, w_down)


def setup_inputs(seed: int = 0) -> dict:
    key = jax.random.key(seed)
    ks = jax.random.split(key, 32)
    L, D = DEPTH, D_MODEL
    f32 = jnp.float32

    def nrm(k, shape, scale):
        return jax.random.normal(k, shape, f32) * scale

    def gain(k, shape):
        return 1.0 + 0.02 * jax.random.normal(k, shape, f32)

    return {
        "x": jax.random.normal(ks[0], (BATCH, SEQ, D), f32),
        "mem": jax.random.normal(ks[1], (BATCH, MEM_LEN, D), f32),
        "norm_mix": gain(ks[2], (L, D)),
        "w_in": nrm(ks[3], (L, D, IN_COLS), D ** -0.5),
        "b_gate": nrm(ks[4], (L, 2, D), 0.02),
        "conv_w": nrm(ks[5], (L, CONV_WIDTH, CONV_DIM), CONV_WIDTH ** -0.5),
        "conv_b": nrm(ks[6], (L, CONV_DIM), 0.02),
        "conv_ln_g": gain(ks[7], (L, CONV_DIM)),
        "conv_ln_b": nrm(ks[8], (L, CONV_DIM), 0.02),
        "w_conv_out": nrm(ks[9], (L, CONV_DIM, D), CONV_DIM ** -0.5),
        "sgu_ln_g": gain(ks[10], (L, SGU_DIM)),
        "sgu_ln_b": nrm(ks[11], (L, SGU_DIM), 0.02),
        "sgu_w": nrm(ks[12], (L, SGU_GROUPS, SGU_CHUNK, SGU_CHUNK), SGU_CHUNK ** -0.5),
        "sgu_b": gain(ks[13], (L, SGU_GROUPS, SGU_CHUNK)),
        "w_sgu_out": nrm(ks[14], (L, SGU_DIM, D), SGU_DIM ** -0.5),
        "w_mix_out": nrm(ks[15], (L, D, D), D ** -0.5),
        "norm_xattn": gain(ks[16], (L, D)),
        "norm_mem": gain(ks[17], (L, D)),
        "w_q": nrm(ks[18], (L, D, D), D ** -0.5),
        "w_kv": nrm(ks[19], (L, D, 2 * D), D ** -0.5),
        "w_xo": nrm(ks[20], (L, D, D), D ** -0.5),
        "norm_ffn": gain(ks[21], (L, D)),
        "w_gu": nrm(ks[22], (L, D, 2 * FFN_HIDDEN), D ** -0.5),
        "w_down": nrm(ks[23], (L, FFN_HIDDEN, D), FFN_HIDDEN ** -0.5),
        "norm_final": gain(ks[24], (D,)),
    }


def reference(x, mem, norm_mix, w_in, b_gate, conv_w, conv_b, conv_ln_g, conv_ln_b, w_conv_out,
              sgu_ln_g, sgu_ln_b, sgu_w, sgu_b, w_sgu_out, w_mix_out,
              norm_xattn, norm_mem, w_q, w_kv, w_xo,
              norm_ffn, w_gu, w_down, norm_final):
    for l in range(DEPTH):
        h = rmsnorm(x, norm_mix[l])
        x = x + mixer_block(h, w_in[l], b_gate[l], conv_w[l], conv_b[l], conv_ln_g[l], conv_ln_b[l],
                            w_conv_out[l], sgu_ln_g[l], sgu_ln_b[l], sgu_w[l], sgu_b[l],
                            w_sgu_out[l], w_mix_out[l])
        h = rmsnorm(x, norm_xattn[l])
        mem_n = rmsnorm(mem, norm_mem[l])
        x = x + memory_cross_attention(h, mem_n, w_q[l], w_kv[l], w_xo[l])
        h = rmsnorm(x, norm_ffn[l])
        x = x + swiglu_ffn(h, w_gu[l], w_down[l])
    return rmsnorm(x, norm_final)
```

```python
import numpy as np
from contextlib import ExitStack
import concourse.bass as bass
import concourse.mybir as mybir
from concourse.bass_utils import run_bass_kernel_spmd

F32 = mybir.dt.float32
BF16 = mybir.dt.bfloat16
AF = mybir.ActivationFunctionType
ALU = mybir.AluOpType

P = 128
D = 1024
KC = 8
T = 512
NST = 4
FF = 2816
FC = 22
MEM = 256
CW = 31
HALO = 30
RMS_EPS = 1e-6
LN_EPS = 1e-5
N_CORES = 8
N_DENSE_TILES = 16
N_EASE2_TILES = 16

C_NMIX, C_BG0, C_BG1, C_CONVB, C_CLNG, C_CLNB, C_SLNG, C_NXAT, C_NMEM, C_NFFN, C_NFIN, C_CONVW = (
    0, 8, 16, 24, 32, 40, 48, 56, 64, 72, 80, 88)
NCOL = 88 + CW * 8


class _Op:
    __slots__ = ("eng", "fn", "deps", "dma_sem", "sig", "sem", "val", "semname", "final")

    def __init__(self, eng, fn, deps, dma_sem, final):
        self.eng = eng
        self.fn = fn
        self.deps = deps
        self.dma_sem = dma_sem
        self.final = final
        self.sig = False
        self.sem = None
        self.val = 0
        self.semname = None


class Sched:
    ENGS = ("pe", "act", "dve", "pool", "sp")

    def __init__(self):
        self.ops = []
        self.last_writer = {}
        self.readers = {}

    def add(self, eng, fn, reads=(), writes=(), dma_sem=None, sem_final=False):
        idx = len(self.ops)
        deps = set()
        for k in reads:
            w = self.last_writer.get(k)
            if w is not None:
                deps.add(w)
        for k in writes:
            w = self.last_writer.get(k)
            if w is not None:
                deps.add(w)
            rd = self.readers.get(k)
            if rd:
                deps.update(rd.values())
        deps.discard(idx)
        rkey = eng if dma_sem is None else ("dma", idx)
        for k in reads:
            self.readers.setdefault(k, {})[rkey] = idx
        for k in writes:
            self.last_writer[k] = idx
            self.readers[k] = {}
        self.ops.append(_Op(eng, fn, sorted(deps), dma_sem, sem_final))
        return idx

    @staticmethod
    def _pe_pe(dop, op):
        return dop.dma_sem is None and op.dma_sem is None and dop.eng == "pe" and op.eng == "pe"

    def emit(self, nc, stack):
        ops = self.ops
        for op in ops:
            for d in op.deps:
                if not self._pe_pe(ops[d], op):
                    ops[d].sig = True
        sems = {}
        counts = {}

        def get_sem(name):
            if name not in sems:
                sems[name] = stack.enter_context(nc.semaphore(name))
            return sems[name]

        for op in ops:
            if op.dma_sem is not None:
                name = "d_" + op.dma_sem
                counts[name] = counts.get(name, 0) + 16
            elif op.sig:
                name = "e_" + op.eng
                counts[name] = counts.get(name, 0) + 1
            else:
                continue
            op.sem = get_sem(name)
            op.val = counts[name]
            op.semname = name
        for op in ops:
            if op.final:
                op.val = counts[op.semname]
        self.counts = counts
        block = stack.enter_context(nc.Block())
        known = {e: {} for e in self.ENGS}

        def run_engine(ename, eobj):
            kn = known[ename]
            for op in ops:
                if op.eng != ename:
                    continue
                waits = {}
                for d in op.deps:
                    dop = ops[d]
                    if dop.sem is None or self._pe_pe(dop, op):
                        continue
                    nm = dop.semname
                    if kn.get(nm, 0) >= dop.val:
                        continue
                    if waits.get(nm, (None, 0))[1] < dop.val:
                        waits[nm] = (dop.sem, dop.val)
                wl = list(waits.values())
                for nm, (s, v) in waits.items():
                    kn[nm] = v
                if op.fn is None:
                    for s, v in wl:
                        eobj.wait_ge(s, v)
                    continue
                attach = None
                if wl and op.dma_sem is None:
                    attach = wl.pop()
                for s, v in wl:
                    eobj.wait_ge(s, v)
                ins = op.fn(eobj)
                if attach is not None:
                    ins._wait_ge(attach[0], attach[1])
                if op.dma_sem is not None:
                    ins.then_inc(op.sem, 16)
                elif op.sig:
                    ins.then_inc(op.sem, 1)

        @block.tensor
        def _(e):
            run_engine("pe", e)

        @block.scalar
        def _(e):
            run_engine("act", e)

        @block.vector
        def _(e):
            run_engine("dve", e)

        @block.gpsimd
        def _(e):
            run_engine("pool", e)

        @block.sync
        def _(e):
            run_engine("sp", e)


def build_program(NB, TPB, NS=4):
    NT = NB * TPB
    nc = bass.Bass("TRN2", target_bir_lowering=False)
    S = Sched()

    def din(name, shape, dt=F32):
        return nc.dram_tensor(name, list(shape), dt, kind="ExternalInput").ap()

    x_d = din("x", [NT * T, D])
    mem_d = din("mem", [NB * MEM, D])
    w_in_d = din("w_in", [D, 6 * D])
    w_co_d = din("w_co", [D, D])
    w_so_d = din("w_so", [D, D])
    w_mo_d = din("w_mo", [D, D])
    w_q_d = din("w_q", [D, D])
    w_kv_d = din("w_kv", [D, 2 * D])
    w_xo_d = din("w_xo", [D, D])
    w_gu_d = din("w_gu", [D, 2 * FF])
    w_dn_d = din("w_dn", [FF, D])
    pcols_d = din("pcols", [P, NCOL])
    ident_d = din("ident", [P, P])
    sguwT_d = din("sguwT", [P, 8 * P])
    brep_d = din("brep", [P, D])
    sgub_d = din("sgub", [1, D])
    out_d = nc.dram_tensor("out", [NT * T, D], F32, kind="ExternalOutput").ap()

    def dscr(name, npieces, nk=KC):
        return nc.dram_tensor(name, [npieces, P, nk, 512], BF16, kind="Internal").ap()

    s_in = dscr("s_in", 12)
    s_co = dscr("s_co", 2)
    s_so = dscr("s_so", 2)
    s_mo = dscr("s_mo", 2)
    s_q = dscr("s_q", 2)
    s_kv = dscr("s_kv", 4)
    s_xo = dscr("s_xo", 2)
    s_gu = dscr("s_gu", 11)
    s_dn = dscr("s_dn", 2, FC)

    st = ExitStack()
    with st:
        def sb(name, shape, dt):
            return st.enter_context(nc.sbuf_tensor("sb_" + name, list(shape), dt))

        pcols = sb("pcols", [P, NCOL], F32)
        identf = sb("identf", [P, P], F32)
        identb = sb("identb", [P, P], BF16)
        onesb = sb("onesb", [P, P], BF16)
        onesf = sb("onesf", [1, P], F32)
        mhalf = sb("mhalf", [P, 1], F32)
        wTm = sb("wTm", [P, 8, P], BF16)
        Cg = sb("Cg", [P, 8, P], F32)
        hT2 = sb("hT2", [P, KC, T], BF16)
        cv2 = sb("cv2", [P, KC, T], BF16)
        xin = sb("xin", [P, NST, D], F32)
        xout = sb("xout", [P, 2, D], F32)
        xTb = sb("xTb", [P, 2, KC, T], F32)
        hT = sb("hT", [P, KC, T], BF16)
        aT = sb("aT", [P, KC, T + HALO], BF16)
        uT = sb("uT", [P, KC, T], BF16)
        vt = sb("vt", [P, NST, D], BF16)
        gv = sb("gv", [P, 2, D], F32)
        ga = sb("ga", [P, KC, T], BF16)
        gb = sb("gb", [P, KC, T], BF16)
        cv = sb("cv", [P, KC, T], BF16)
        sg = sb("sg", [P, KC, T], BF16)
        sqb = sb("sqb", [P, 2, T], BF16)
        sgt = sb("sgt", [P, 2, T], BF16)
        ftmp = sb("ftmp", [P, 2, T], F32)
        nmb = sb("nmb", [P, T], F32)
        stat6 = sb("stat6", [P, 2, 2, 6], F32)
        mv = sb("mv", [P, 2, 2], F32)
        smal = sb("smal", [P, 2, 4], F32)
        KT = sb("KT", [P, KC, MEM], BF16)
        Vm = sb("Vm", [P, 2, D], BF16)
        memT = sb("memT", [P, KC, MEM], BF16)
        wring = sb("wring", [P, NS, KC, 512], BF16)
        pb = [st.enter_context(nc.psum_tensor(f"pb{i}", [P, 512], F32)) for i in range(8)]

        wTf = xout[:, 0, :].rearrange("p (g t) -> p g t", g=8)
        brep = xout[:, 1, :]
        memTf = xout[:].rearrange("p s (c m) -> p (s c) m", m=MEM)
        XOUT_KEYS = [("xout", 0), ("xout", 1)]
        sgub = gv[0:1, 1, :]

        cnt = {"mm": 0, "st": 0, "sq": 0, "sgt": 0, "ft": 0, "ws": 0, "dg": 0, "gv": 0, "xo": 0}

        def bank():
            i = cnt["mm"] % 4
            cnt["mm"] += 1
            return i

        def sbank():
            i = 6 + cnt["st"] % 2
            cnt["st"] += 1
            return i

        def rot(name, n):
            i = cnt[name] % n
            cnt[name] += 1
            return i

        def MM(bi, out_ap, lhsT, rhs, start, stop, reads):
            S.add("pe", lambda e: e.matmul(out_ap, lhsT=lhsT, rhs=rhs, start=start, stop=stop),
                  reads=reads, writes=[("pb", bi)])

        def ACT(out, in_, func, reads, writes, bias=None, scale=None):
            kw = {}
            if bias is not None:
                kw["bias"] = bias
            if scale is not None:
                kw["scale"] = scale
            S.add("act", lambda e: e.activation(out=out, in_=in_, func=func, **kw), reads=reads, writes=writes)

        def TT(eng, out, in0, in1, op, reads, writes):
            S.add(eng, lambda e: e.tensor_tensor(out=out, in0=in0, in1=in1, op=op), reads=reads, writes=writes)

        def STT(out, in0, scalar, in1, op0, op1, reads, writes):
            S.add("dve", lambda e: e.scalar_tensor_tensor(out=out, in0=in0, scalar=scalar, in1=in1, op0=op0, op1=op1),
                  reads=reads, writes=writes)

        def TS(eng, out, in0, s1, s2, op0, op1, reads, writes):
            if op1 is None:
                S.add(eng, lambda e: e.tensor_scalar(out=out, in0=in0, scalar1=s1, scalar2=None, op0=op0),
                      reads=reads, writes=writes)
            else:
                S.add(eng, lambda e: e.tensor_scalar(out=out, in0=in0, scalar1=s1, scalar2=s2, op0=op0, op1=op1),
                      reads=reads, writes=writes)

        def COPY(eng, out, in_, reads, writes):
            if eng == "act":
                ACT(out, in_, AF.Copy, reads, writes)
            else:
                S.add(eng, lambda e: e.tensor_copy(out=out, in_=in_), reads=reads, writes=writes)

        def col(off, c):
            return pcols[:, off + c:off + c + 1]

        def kcview(ap):
            return ap.rearrange("(kc p) c -> p kc c", p=P)

        prep = {}

        def prep_add(group, out_ap, in_ap):
            prep.setdefault(group, []).append((out_ap, in_ap))

        for j in range(4):
            prep_add("inA", s_in[j][:, :, 0:256], kcview(w_in_d[:, j * 256:(j + 1) * 256]))
            prep_add("inA", s_in[j][:, :, 256:512], kcview(w_in_d[:, D + j * 256:D + (j + 1) * 256]))
        for n in (6, 7):
            prep_add("inB", s_in[n], kcview(w_in_d[:, n * 512:(n + 1) * 512]))
        for n in (4, 5):
            prep_add("inC", s_in[n], kcview(w_in_d[:, n * 512:(n + 1) * 512]))
        for n in (8, 9, 10, 11):
            prep_add("inD", s_in[n], kcview(w_in_d[:, n * 512:(n + 1) * 512]))
        for nm, sc, wd, npc in (("co", s_co, w_co_d, 2), ("so", s_so, w_so_d, 2), ("mo", s_mo, w_mo_d, 2),
                                ("q", s_q, w_q_d, 2), ("kv", s_kv, w_kv_d, 4), ("xo", s_xo, w_xo_d, 2)):
            for n in range(npc):
                prep_add(nm, sc[n], kcview(wd[:, n * 512:(n + 1) * 512]))
        for j in range(11):
            grp = "gu%d" % (j // 4)
            prep_add(grp, s_gu[j][:, :, 0:256], kcview(w_gu_d[:, j * 256:(j + 1) * 256]))
            prep_add(grp, s_gu[j][:, :, 256:512], kcview(w_gu_d[:, FF + j * 256:FF + (j + 1) * 256]))
        for n in range(2):
            prep_add("dn%d" % n, s_dn[n], kcview(w_dn_d[:, n * 512:(n + 1) * 512]))
        prep_done = set()

        def ensure_prep(group):
            if group in prep_done:
                return
            prep_done.add(group)
            for i, (o, a) in enumerate(prep[group]):
                S.add("pool", lambda e, o=o, a=a: e.dma_start(out=o, in_=a),
                      writes=[("scr", group, i)], dma_sem="scr_" + group, sem_final=True)

        def scr_keys(group):
            return [("scr", group, i) for i in range(len(prep[group]))]

        def wload(src_ap, nk, group):
            ensure_prep(group)
            s = rot("ws", NS)
            S.add("sp", lambda e: e.dma_start(out=wring[:, s, 0:nk, :], in_=src_ap),
                  reads=scr_keys(group), writes=[("ws", s)], dma_sem="ws%d" % s)
            return s

        for dst, src, key in ((pcols[:], pcols_d[:, :], "pcols"), (identf[:], ident_d[:, :], "identf"),
                              (sgub, sgub_d[:, :], ("gv", 1))):
            S.add("sp", lambda e, dst=dst, src=src: e.dma_start(out=dst, in_=src), writes=[key],
                  dma_sem="setup", sem_final=True)
        S.add("sp", lambda e: e.dma_start(out=xout[:, 0, :], in_=sguwT_d[:, :]), writes=[("xout", 0)],
              dma_sem="setup", sem_final=True)
        S.add("sp", lambda e: e.dma_start(out=xout[:, 1, :], in_=brep_d[:, :]), writes=[("xout", 1)],
              dma_sem="setup", sem_final=True)

        def xload(ti):
            for s4 in range(NST):
                r0 = (ti * NST + s4) * P
                S.add("sp", lambda e, s4=s4, r0=r0: e.dma_start(out=xin[:, s4, :], in_=x_d[r0:r0 + P, :]),
                      writes=[("xin", s4)], dma_sem="xin%d" % s4)

        xload(0)

        COPY("dve", identb[:], identf[:], ["identf"], ["identb"])
        S.add("dve", lambda e: e.memset(onesb[:], 1.0), writes=["onesb"])
        S.add("dve", lambda e: e.memset(onesf[:], 1.0), writes=["onesf"])
        S.add("dve", lambda e: e.memset(mhalf[:], -0.5), writes=["mhalf"])
        S.add("pool", lambda e: e.affine_select(out=wTm[:], in_=wTf, pattern=[[0, 8], [1, P]], compare_op=ALU.is_ge,
                                                fill=0.0, base=0, channel_multiplier=-1),
              reads=[("xout", 0)], writes=["wTm"])
        S.add("pool", lambda e: e.affine_select(out=wTf, in_=wTf, pattern=[[0, 8], [1, P]], compare_op=ALU.is_ge,
                                                fill=0.0, base=0, channel_multiplier=-1),
              reads=[("xout", 0)], writes=[("xout", 0)])
        ensure_prep("inA")
        ensure_prep("inB")
        ensure_prep("kv")
        ensure_prep("inC")
        ensure_prep("inD")
        for g in range(8):
            bi = bank()
            o = pb[bi][:, 0:P]
            MM(bi, o, brep[:, g * P:(g + 1) * P], wTf[:, g, :], True, False, [("xout", 0), ("xout", 1)])
            MM(bi, o, onesf[0:1, :], sgub[0:1, g * P:(g + 1) * P], False, True, ["onesf", ("gv", 1)])
            COPY("dve", Cg[:, g, :], o, [("pb", bi)], ["Cg"])

        pend = []

        def defer(fn):
            pend.append([fn, 0])

        def tick(flush=False):
            keep = []
            for it in pend:
                it[1] += 1
                if flush or it[1] >= 2:
                    it[0]()
                else:
                    keep.append(it)
            pend[:] = keep

        def sq_accum(sbi, src_ap, src_key, N, first, last, deferred=True):
            i = rot("sq", 2)
            dst = sqb[:, i, 0:N]
            ACT(dst, src_ap, AF.Square, [src_key], [("sqb", i)])

            def f():
                MM(sbi, pb[sbi][:, 0:N], onesb[:], dst, first, last, ["onesb", ("sqb", i)])
            if deferred:
                defer(f)
            else:
                f()

        def rstd_from_ss(sbi, N, eps):
            i = rot("ft", 2)
            tmp = ftmp[:, i, 0:N]
            ACT(tmp, pb[sbi][:, 0:N], AF.Ln, [("pb", sbi)], [("ft", i)], bias=eps, scale=1.0 / D)
            ACT(pb[sbi][:, 0:N], tmp, AF.Exp, [("ft", i)], [("pb", sbi)], scale=-0.5)

        def norm_apply(sbi, N, src, dst, goff):
            for c in range(KC):
                STT(dst[c][0], src[c][0], col(goff, c), pb[sbi][:, 0:N], ALU.mult, ALU.mult,
                    [src[c][1], ("pb", sbi), "pcols"], [dst[c][1]])

        hT_c = [(hT[:, c, :], ("hT", c)) for c in range(KC)]

        def proj_fm(slot, nblk, rhs_list, nk, epilogue, k0=0):
            for m in range(nblk):
                bi = bank()
                for kk in range(nk):
                    MM(bi, pb[bi][:], wring[:, slot, kk, m * P:(m + 1) * P], rhs_list[k0 + kk][0],
                       kk == 0, kk == nk - 1, [("ws", slot), rhs_list[k0 + kk][1]])
                epilogue(m, bi)
                tick()

        def mem_part1(b):
            sbi = sbank()
            for mc in range(2):
                r0 = b * MEM + mc * P
                S.add("sp", lambda e, r0=r0: e.dma_start(out=gv[:, 0, :], in_=mem_d[r0:r0 + P, :]),
                      writes=[("gv", 0)], dma_sem="memld")
                for q in range(2):
                    bi = bank()
                    for c4 in range(4):
                        c = q * 4 + c4
                        S.add("pe", lambda e, bi=bi, c4=c4, c=c: e.transpose(
                            out=pb[bi][:, c4 * P:(c4 + 1) * P], in_=gv[:, 0, c * P:(c + 1) * P], identity=identf[:]),
                            reads=[("gv", 0), "identf"], writes=[("pb", bi)])
                    COPY("dve", memTf[:, q * 4:(q + 1) * 4, mc * P:(mc + 1) * P],
                         pb[bi][:].rearrange("p (c m) -> p c m", c=4), [("pb", bi)], XOUT_KEYS)
            for c in range(KC):
                sq_accum(sbi, memTf[:, c, :], ("xout", c // 4), MEM, c == 0, c == KC - 1, deferred=False)
            return sbi

        def mem_part2(sbi):
            rstd_from_ss(sbi, MEM, RMS_EPS)
            src = [(memTf[:, c, :], ("xout", c // 4)) for c in range(KC)]
            dst = [(memT[:, c, :], ("memT", c)) for c in range(KC)]
            norm_apply(sbi, MEM, src, dst, C_NMEM)
            for n in range(2):
                slot = wload(s_kv[n], KC, "kv")
                for m in range(4):
                    c = n * 4 + m
                    bi = bank()
                    for kk in range(KC):
                        MM(bi, pb[bi][:, 0:MEM], wring[:, slot, kk, m * P:(m + 1) * P], memT[:, kk, :],
                           kk == 0, kk == KC - 1, [("ws", slot), ("memT", kk)])
                    COPY("act", KT[:, c, :], pb[bi][:, 0:MEM], [("pb", bi)], [("KT", c)])
            for n in range(2):
                slot = wload(s_kv[2 + n], KC, "kv")
                for mc in range(2):
                    bi = bank()
                    for kk in range(KC):
                        MM(bi, pb[bi][:], memT[:, kk, mc * P:(mc + 1) * P], wring[:, slot, kk, :],
                           kk == 0, kk == KC - 1, [("ws", slot), ("memT", kk)])
                    COPY("act", Vm[:, mc, n * 512:(n + 1) * 512], pb[bi][:], [("pb", bi)], [("Vm", mc)])

        GA = [(ga[:, c, :], ("ga", c)) for c in range(KC)]
        GB = [(gb[:, c, :], ("gb", c)) for c in range(KC)]
        CV = [(cv[:, c, :], ("cv", c)) for c in range(KC)]
        SG = [(sg[:, c, :], ("sg", c)) for c in range(KC)]
        UT = [(uT[:, c, :], ("uT", c)) for c in range(KC)]
        HID = (GA + GB + CV)[:FC]
        vt_e = vt[:].rearrange("p s (h t) -> p (s h) t", t=T)
        ET = [(vt_e[:, i, :], ("vt", i // 2)) for i in range(8)]
        XT = [[(xTb[:, par, c, :], ("xT", par, c)) for c in range(KC)] for par in range(2)]
        front_sb = {}

        def residual_add_and_stats(xc, c, bi, sbi):
            TT("dve", xc[c][0], pb[bi][:], xc[c][0], ALU.add, [("pb", bi), xc[c][1]], [xc[c][1]])
            sq_accum(sbi, xc[c][0], xc[c][1], T, c == 0, c == KC - 1)

        def front_T(ti):
            par = ti % 2
            for s4 in range(NST):
                for q in range(2):
                    bi = bank()
                    for c4 in range(4):
                        c = q * 4 + c4
                        S.add("pe", lambda e, bi=bi, c4=c4, c=c, s4=s4: e.transpose(
                            out=pb[bi][:, c4 * P:(c4 + 1) * P], in_=xin[:, s4, c * P:(c + 1) * P], identity=identf[:]),
                            reads=[("xin", s4), "identf"], writes=[("pb", bi)])
                    COPY("act" if (q == 0 or ti < N_EASE2_TILES) else "dve", xTb[:, par, q * 4:(q + 1) * 4, s4 * P:(s4 + 1) * P],
                         pb[bi][:].rearrange("p (c m) -> p c m", c=4), [("pb", bi)],
                         [("xT", par, q * 4 + i) for i in range(4)])

        def front_ss(ti):
            par = ti % 2
            sbi = sbank()
            front_sb[ti] = sbi
            for c in range(KC):
                sq_accum(sbi, XT[par][c][0], XT[par][c][1], T, c == 0, c == KC - 1, deferred=False)

        def front_norm(ti):
            par = ti % 2
            sbi = front_sb[ti]
            rstd_from_ss(sbi, T, RMS_EPS)
            norm_apply(sbi, T, XT[par], hT_c, C_NMIX)

        def back_T_groups(ti):
            par = ti % 2
            groups = []
            for s4 in range(NST):
                box = {}
                for q in range(2):
                    def g(s4=s4, q=q, box=box):
                        if q == 0:
                            box["xs"] = rot("xo", 2)
                        xs = box["xs"]
                        bi = bank()
                        for c4 in range(4):
                            c = q * 4 + c4
                            S.add("pe", lambda e, bi=bi, c4=c4, c=c: e.transpose(
                                out=pb[bi][:, c4 * P:(c4 + 1) * P], in_=xTb[:, par, c, s4 * P:(s4 + 1) * P],
                                identity=identf[:]),
                                reads=[("xT", par, c), "identf"], writes=[("pb", bi)])
                        COPY("act" if (q == 0 or ti < N_EASE2_TILES) else "dve", xout[:, xs, q * 512:(q + 1) * 512],
                             pb[bi][:], [("pb", bi)], [("xout", xs)])
                        if q == 1:
                            r0 = (ti * NST + s4) * P
                            S.add("pool", lambda e: e.dma_start(out=out_d[r0:r0 + P, :], in_=xout[:, xs, :]),
                                  reads=[("xout", xs)], writes=[("outd", xs)], dma_sem="xo%d" % xs)
                    groups.append(g)
            return groups

        def back_T(ti):
            for g in back_T_groups(ti):
                g()

        def front_T_groups(ti):
            par = ti % 2
            groups = []
            for s4 in range(NST):
                for q in range(2):
                    def g(s4=s4, q=q):
                        bi = bank()
                        for c4 in range(4):
                            c = q * 4 + c4
                            S.add("pe", lambda e, bi=bi, c4=c4, c=c: e.transpose(
                                out=pb[bi][:, c4 * P:(c4 + 1) * P], in_=xin[:, s4, c * P:(c + 1) * P], identity=identf[:]),
                                reads=[("xin", s4), "identf"], writes=[("pb", bi)])
                        COPY("act" if (q == 0 or ti < N_EASE2_TILES) else "dve",
                             xTb[:, par, q * 4:(q + 1) * 4, s4 * P:(s4 + 1) * P],
                             pb[bi][:].rearrange("p (c m) -> p c m", c=4), [("pb", bi)],
                             [("xT", par, q * 4 + i) for i in range(4)])
                    groups.append(g)
            return groups

        HT2 = [(hT2[:, c, :], ("hT2", c)) for c in range(KC)]
        CV2 = [(cv2[:, c, :], ("cv2", c)) for c in range(KC)]
        convq = []

        def pump(n=None):
            k = len(convq) if n is None else min(n, len(convq))
            for _ in range(k):
                convq.pop(0)()

        def front_ss_chunk(ti, c):
            par = ti % 2
            if ti not in front_sb:
                front_sb[ti] = sbank()
            sq_accum(front_sb[ti], XT[par][c][0], XT[par][c][1], T, c == 0, c == KC - 1, deferred=True)

        def front_norm2(ti):
            par = ti % 2
            sbi = front_sb[ti]
            rstd_from_ss(sbi, T, RMS_EPS)
            norm_apply(sbi, T, XT[par], HT2, C_NMIX)

        def pre_a(ti):
            pos = ti % TPB
            if pos == 0:
                S.add("pool", lambda e: e.memset(aT[:, :, 0:HALO], 0.0), writes=[("aT", c) for c in range(KC)])
            else:
                S.add("pool", lambda e: e.tensor_copy(out=aT[:, :, 0:HALO], in_=aT[:, :, T:T + HALO]),
                      reads=[("aT", c) for c in range(KC)], writes=[("aT", c) for c in range(KC)])
            for j in range(4):
                slot = wload(s_in[j], KC, "inA")
                for i2 in range(2):
                    c = 2 * j + i2
                    bg = bank()
                    for kk in range(KC):
                        MM(bg, pb[bg][:], wring[:, slot, kk, (2 + i2) * P:(3 + i2) * P], hT2[:, kk, :],
                           kk == 0, kk == KC - 1, [("ws", slot), ("hT2", kk)])
                    si = rot("sgt", 2)
                    ACT(sgt[:, si, :], pb[bg][:], AF.Sigmoid, [("pb", bg)], [("sgt", si)])
                    bv = bank()
                    for kk in range(KC):
                        MM(bv, pb[bv][:], wring[:, slot, kk, i2 * P:(i2 + 1) * P], hT2[:, kk, :],
                           kk == 0, kk == KC - 1, [("ws", slot), ("hT2", kk)])
                    TT("dve", aT[:, c, HALO:HALO + T], pb[bv][:], sgt[:, si, :], ALU.mult,
                       [("pb", bv), ("sgt", si)], [("aT", c)])
            if ti == 0:
                for c in range(KC):
                    bi = bank()
                    for k in range(CW):
                        d = rot("dg", 16)
                        dap = sg[:, d // 4, (d % 4) * P:(d % 4 + 1) * P]
                        S.add("dve", lambda e, dap=dap, k=k, c=c: e.tensor_scalar(
                            out=dap, in0=identb[:], scalar1=col(C_CONVW, k * 8 + c), scalar2=None, op0=ALU.mult),
                            reads=["identb", "pcols"], writes=[("sgd", d)])
                        MM(bi, pb[bi][:], dap, aT[:, c, k:k + T], k == 0, k == CW - 1, [("sgd", d), ("aT", c)])
                    ACT(cv2[:, c, :], pb[bi][:], AF.Identity, [("pb", bi), "pcols"], [("cv2", c)], bias=col(C_CONVB, c))
                return
            for cp in range(0, KC, 2):
                for k in range(CW):
                    for i2 in range(2):
                        c = cp + i2
                        cb = 4 + i2

                        def f(c=c, cb=cb, k=k):
                            if k == 0:
                                TS("dve", pb[cb][:], aT[:, c, 0:T], col(C_CONVW, c), None, ALU.mult, None,
                                   [("aT", c), "pcols"], [("pb", cb)])
                            else:
                                STT(pb[cb][:], aT[:, c, k:k + T], col(C_CONVW, k * 8 + c), pb[cb][:], ALU.mult, ALU.add,
                                    [("aT", c), "pcols", ("pb", cb)], [("pb", cb)])
                        convq.append(f)
                for i2 in range(2):
                    c = cp + i2
                    cb = 4 + i2

                    def g(c=c, cb=cb):
                        ACT(cv2[:, c, :], pb[cb][:], AF.Identity, [("pb", cb), "pcols"], [("cv2", c)], bias=col(C_CONVB, c))
                    convq.append(g)

        front_T(0)
        front_ss(0)
        front_norm2(0)
        pre_a(0)

        for ti in range(NT):
            b, pos = divmod(ti, TPB)
            par = ti % 2
            xc = XT[par]
            dense_tile = ti < N_DENSE_TILES
            if ti + 1 < NT:
                xload(ti + 1)
            if pos == 0:
                mem_sbi = mem_part1(b)

            slotv = [wload(s_in[6], KC, "inB"), wload(s_in[7], KC, "inB")]
            for s4 in range(NST):
                gi = rot("gv", 2)
                for half in range(2):
                    bi = bank()
                    for kk in range(KC):
                        MM(bi, pb[bi][:], hT2[:, kk, s4 * P:(s4 + 1) * P], wring[:, slotv[half], kk, :],
                           kk == 0, kk == KC - 1, [("ws", slotv[half]), ("hT2", kk)])
                    ACT(gv[:, gi, half * 512:(half + 1) * 512], pb[bi][:], AF.Gelu_apprx_tanh,
                        [("pb", bi)], [("gv", gi)])
                for half in range(2):
                    S.add("dve", lambda e, gi=gi, half=half: e.bn_stats(
                        out=stat6[:, gi, half, :], in_=gv[:, gi, half * 512:(half + 1) * 512]),
                        reads=[("gv", gi)], writes=[("stat6", gi)])
                S.add("dve", lambda e, gi=gi: e.bn_aggr(out=mv[:, gi, :], in_=stat6[:, gi].rearrange("p a b -> p (a b)")),
                      reads=[("stat6", gi)], writes=[("mv", gi)])
                TS("dve", smal[:, gi, 0:1], mv[:, gi, 1:2], LN_EPS, None, ALU.add, None, [("mv", gi)], [("smal", gi)])
                TS("dve", smal[:, gi, 2:3], mv[:, gi, 0:1], -1.0, None, ALU.mult, None, [("mv", gi)], [("smal", gi)])
                TT("pool", smal[:, gi, 1:2], smal[:, gi, 0:1], mhalf[:], ALU.pow, [("smal", gi), "mhalf"], [("smal", gi)])
                TS("pool", vt[:, s4, :], gv[:, gi, :], smal[:, gi, 2:3], smal[:, gi, 1:2], ALU.add, ALU.mult,
                   [("gv", gi), ("smal", gi)], [("vt", s4)])
                pump(5)
            if pos == 0:
                mem_part2(mem_sbi)
            btq = back_T_groups(ti - 1) if ti > 0 else []
            for n in range(2):
                slot = wload(s_in[4 + n], KC, "inC")

                def ep(m, bi, n=n):
                    c = n * 4 + m
                    ACT(uT[:, c, :], pb[bi][:], AF.Gelu_apprx_tanh, [("pb", bi)], [("uT", c)])
                    pump(3)
                    if btq:
                        btq.pop(0)()
                proj_fm(slot, 4, HT2, KC, ep)
            while btq:
                btq.pop(0)()
            def sgu_group(g):
                bi = bank()
                for s4 in range(NST):
                    MM(bi, pb[bi][:, s4 * P:(s4 + 1) * P], vt[:, s4, g * P:(g + 1) * P], wTm[:, g, :], True, True,
                       [("vt", s4), "wTm"])
                v3 = pb[bi][:].rearrange("p (s t) -> p s t", s=NST)
                STT(v3, v3, col(C_SLNG, g), Cg[:, g, :].unsqueeze(1).broadcast_to([P, NST, P]), ALU.mult, ALU.add,
                    [("pb", bi), "pcols", "Cg"], [("pb", bi)])
                TT("dve", sg[:, g, :], pb[bi][:], uT[:, g, :], ALU.mult, [("pb", bi), ("uT", g)], [("sg", g)])

            for n in range(4):
                slot = wload(s_in[8 + n], KC, "inD")

                def ep(m, bi, n=n):
                    c = (n % 2) * 4 + m
                    dst, off = (GA, C_BG0) if n < 2 else (GB, C_BG1)
                    ACT(dst[c][0], pb[bi][:], AF.Sigmoid, [("pb", bi), "pcols"], [dst[c][1]], bias=col(off, c))
                    pump(1)
                    gidx = n * 4 + m
                    if dense_tile and gidx % 2 == 1:
                        sgu_group(gidx // 2)
                proj_fm(slot, 4, HT2, KC, ep)

            if ti == 0:
                for g_ in ("co", "so", "mo", "q", "xo"):
                    ensure_prep(g_)
            pump()
            if not dense_tile:
                for g in range(8):
                    sgu_group(g)

            s1b = sbank()
            s2b = sbank()
            for g in range(8):
                c = g
                i = rot("sq", 2)
                ACT(sqb[:, i, :], cv2[:, c, :], AF.Square, [("cv2", c)], [("sqb", i)])

                def fst(c=c, i=i):
                    MM(s1b, pb[s1b][:], onesb[:], cv2[:, c, :], c == 0, c == KC - 1, ["onesb", ("cv2", c)])
                    MM(s2b, pb[s2b][:], onesb[:], sqb[:, i, :], c == 0, c == KC - 1, ["onesb", ("sqb", i)])
                if g >= 1:
                    prev_fst()
                prev_fst = fst
            prev_fst()
            i_m2 = rot("ft", 2)
            ACT(ftmp[:, i_m2, :], pb[s1b][:], AF.Square, [("pb", s1b)], [("ft", i_m2)], scale=1.0 / D)
            STT(ftmp[:, i_m2, :], pb[s2b][:], 1.0 / D, ftmp[:, i_m2, :], ALU.mult, ALU.subtract,
                [("pb", s2b), ("ft", i_m2)], [("ft", i_m2)])
            ACT(ftmp[:, i_m2, :], ftmp[:, i_m2, :], AF.Ln, [("ft", i_m2)], [("ft", i_m2)], bias=LN_EPS)
            ACT(pb[s2b][:], ftmp[:, i_m2, :], AF.Exp, [("ft", i_m2)], [("pb", s2b)], scale=-0.5)
            ACT(nmb[:], pb[s1b][:], AF.Copy, [("pb", s1b)], ["nmb"], scale=-1.0 / D)

            ftq = front_T_groups(ti + 1) if ti + 1 < NT else []
            for n in range(2):
                slot = wload(s_so[n], KC, "so")

                def ep(m, bi, n=n):
                    c = n * 4 + m
                    TT("dve", gb[:, c, :], pb[bi][:], gb[:, c, :], ALU.mult, [("pb", bi), ("gb", c)], [("gb", c)])
                    if ftq:
                        ftq.pop(0)()
                proj_fm(slot, 4, SG, KC, ep)
                if n == 0:
                    for c in range(KC):
                        i = rot("ft", 2)
                        TT("pool", ftmp[:, i, :], cv2[:, c, :], nmb[:], ALU.add,
                           [("cv2", c), "nmb"], [("ft", i)])
                        TT("dve", ftmp[:, i, :], ftmp[:, i, :], pb[s2b][:], ALU.mult,
                           [("ft", i), ("pb", s2b)], [("ft", i)])
                        ACT(hT[:, c, :], ftmp[:, i, :], AF.Silu, [("ft", i), "pcols"], [("hT", c)],
                            bias=col(C_CLNB, c), scale=col(C_CLNG, c))
                    if ti == 0:
                        ensure_prep("gu0")
            while ftq:
                ftq.pop(0)()
            for n in range(2):
                slot = wload(s_co[n], KC, "co")

                def ep(m, bi, n=n):
                    c = n * 4 + m
                    TT("dve", ga[:, c, :], pb[bi][:], ga[:, c, :], ALU.mult, [("pb", bi), ("ga", c)], [("ga", c)])
                    TT("pool", uT[:, c, :], ga[:, c, :], gb[:, c, :], ALU.add, [("ga", c), ("gb", c)], [("uT", c)])
                    if ti + 1 < NT:
                        front_ss_chunk(ti + 1, c)
                proj_fm(slot, 4, hT_c, KC, ep)
            tick(flush=True)
            if ti + 1 < NT:
                front_norm2(ti + 1)

            if ti == 0:
                for g_ in ("gu0", "gu1", "gu2"):
                    ensure_prep(g_)

            sbi = sbank()
            for n in range(2):
                slot = wload(s_mo[n], KC, "mo")

                def ep(m, bi, n=n):
                    residual_add_and_stats(xc, n * 4 + m, bi, sbi)
                proj_fm(slot, 4, UT, KC, ep)
            tick(flush=True)

            rstd_from_ss(sbi, T, RMS_EPS)
            norm_apply(sbi, T, xc, hT_c, C_NXAT)
            if ti + 1 < NT:
                pre_a(ti + 1)
            for n in range(2):
                slot = wload(s_q[n], KC, "q")

                def ep(m, bi, n=n):
                    c = n * 4 + m
                    ACT(cv[:, c, :], pb[bi][:], AF.Copy, [("pb", bi)], [("cv", c)], scale=1.0 / 16.0)
                    pump(2)
                proj_fm(slot, 4, hT_c, KC, ep)
            for h in range(4):
                for mc in range(2):
                    bi = bank()
                    for dc in range(2):
                        MM(bi, pb[bi][:], KT[:, 2 * h + dc, mc * P:(mc + 1) * P], cv[:, 2 * h + dc, :], dc == 0, dc == 1,
                           [("KT", 2 * h + dc), ("cv", 2 * h + dc)])
                    e_ap, e_key = ET[2 * h + mc]
                    ACT(e_ap, pb[bi][:], AF.Exp, [("pb", bi)], [e_key])
                bd = bank()
                for mc in range(2):
                    MM(bd, pb[bd][:], onesb[:], ET[2 * h + mc][0], mc == 0, mc == 1, ["onesb", ET[2 * h + mc][1]])
                i = rot("ft", 2)
                ACT(ftmp[:, i, :], pb[bd][:], AF.Ln, [("pb", bd)], [("ft", i)])
                ACT(ftmp[:, i, :], ftmp[:, i, :], AF.Exp, [("ft", i)], [("ft", i)], scale=-1.0)
                for dc in range(2):
                    c = 2 * h + dc
                    bo = bank()
                    for mc in range(2):
                        MM(bo, pb[bo][:], Vm[:, mc, c * P:(c + 1) * P], ET[2 * h + mc][0], mc == 0, mc == 1,
                           [("Vm", mc), ET[2 * h + mc][1]])
                    TT("dve", sg[:, c, :], pb[bo][:], ftmp[:, i, :], ALU.mult, [("pb", bo), ("ft", i)], [("sg", c)])
                pump(4)
            sbi = sbank()
            for n in range(2):
                slot = wload(s_xo[n], KC, "xo")

                def ep(m, bi, n=n):
                    residual_add_and_stats(xc, n * 4 + m, bi, sbi)
                    pump(2)
                proj_fm(slot, 4, SG, KC, ep)
            tick(flush=True)

            rstd_from_ss(sbi, T, RMS_EPS)
            norm_apply(sbi, T, xc, hT_c, C_NFFN)

            if ti == 0:
                for g_ in ("dn0", "dn1"):
                    ensure_prep(g_)
            for j in range(11):
                slot = wload(s_gu[j], KC, "gu%d" % (j // 4))
                for i2 in range(2):
                    fc = 2 * j + i2
                    bg = bank()
                    for kk in range(KC):
                        MM(bg, pb[bg][:], wring[:, slot, kk, i2 * P:(i2 + 1) * P], hT[:, kk, :],
                           kk == 0, kk == KC - 1, [("ws", slot), ("hT", kk)])
                    si = rot("sgt", 2)
                    ACT(sgt[:, si, :], pb[bg][:], AF.Silu, [("pb", bg)], [("sgt", si)])
                    bu = bank()
                    for kk in range(KC):
                        MM(bu, pb[bu][:], wring[:, slot, kk, (2 + i2) * P:(3 + i2) * P], hT[:, kk, :],
                           kk == 0, kk == KC - 1, [("ws", slot), ("hT", kk)])
                    TT("dve", HID[fc][0], pb[bu][:], sgt[:, si, :], ALU.mult, [("pb", bu), ("sgt", si)], [HID[fc][1]])
                    pump(5)
            sbi = sbank()
            for n in range(2):
                bks = [bank() for _ in range(4)]
                for kg, (k0, k1) in enumerate(((0, 8), (8, 16), (16, 22))):
                    slot = wload(s_dn[n][:, k0:k1, :], k1 - k0, "dn%d" % n)
                    for m in range(4):
                        for kk in range(k1 - k0):
                            MM(bks[m], pb[bks[m]][:], wring[:, slot, kk, m * P:(m + 1) * P], HID[k0 + kk][0],
                               k0 + kk == 0, k0 + kk == FC - 1, [("ws", slot), HID[k0 + kk][1]])
                        pump(2 if kg < 2 else (1 if m < 2 else 0))
                for m in range(4):
                    residual_add_and_stats(xc, n * 4 + m, bks[m], sbi)
                    tick()
            tick(flush=True)

            rstd_from_ss(sbi, T, RMS_EPS)
            norm_apply(sbi, T, xc, xc, C_NFIN)


        back_T(NT - 1)

        S.add("pool", None, reads=[("outd", 0), ("outd", 1)])
        S.emit(nc, st)
    return nc, S


def _cols(v):
    return np.ascontiguousarray(np.asarray(v, np.float32).reshape(-1, P).T)


def make_in_maps(inputs, n_cores, NB, seq):
    f = lambda k: np.asarray(inputs[k], np.float32)
    x = f("x")
    mem = f("mem")
    pc = np.zeros((P, NCOL), np.float32)
    bg = f("b_gate")[0]
    for off, v in ((C_NMIX, f("norm_mix")[0]), (C_BG0, bg[0]), (C_BG1, bg[1]), (C_CONVB, f("conv_b")[0]),
                   (C_CLNG, f("conv_ln_g")[0]), (C_CLNB, f("conv_ln_b")[0]), (C_SLNG, f("sgu_ln_g")[0]),
                   (C_NXAT, f("norm_xattn")[0]), (C_NMEM, f("norm_mem")[0]), (C_NFFN, f("norm_ffn")[0]),
                   (C_NFIN, f("norm_final"))):
        pc[:, off:off + 8] = _cols(v)
    cw = f("conv_w")[0]
    for k in range(CW):
        pc[:, C_CONVW + k * 8:C_CONVW + (k + 1) * 8] = _cols(cw[k])
    sguw = f("sgu_w")[0]
    sguwT = np.ascontiguousarray(sguw.transpose(2, 0, 1)).reshape(P, 8 * P)
    brep = np.ascontiguousarray(np.broadcast_to(f("sgu_ln_b")[0][None, :], (P, D)))
    sgub = np.ascontiguousarray(f("sgu_b")[0].reshape(1, D))
    common = {
        "w_in": np.ascontiguousarray(f("w_in")[0]), "w_co": np.ascontiguousarray(f("w_conv_out")[0]),
        "w_so": np.ascontiguousarray(f("w_sgu_out")[0]), "w_mo": np.ascontiguousarray(f("w_mix_out")[0]),
        "w_q": np.ascontiguousarray(f("w_q")[0]), "w_kv": np.ascontiguousarray(f("w_kv")[0]),
        "w_xo": np.ascontiguousarray(f("w_xo")[0]), "w_gu": np.ascontiguousarray(f("w_gu")[0]),
        "w_dn": np.ascontiguousarray(f("w_down")[0]),
        "pcols": pc, "ident": np.eye(P, dtype=np.float32), "sguwT": sguwT, "brep": brep, "sgub": sgub,
    }
    maps = []
    for ci in range(n_cores):
        m = dict(common)
        m["x"] = np.ascontiguousarray(x[ci * NB:(ci + 1) * NB].reshape(NB * seq, D))
        m["mem"] = np.ascontiguousarray(mem[ci * NB:(ci + 1) * NB].reshape(NB * MEM, D))
        maps.append(m)
    return maps


_CACHE = {}


def kernel(**inputs):
    x = np.asarray(inputs["x"])
    B, seq, _ = x.shape
    NB = B // N_CORES
    TPB = seq // T
    key = (NB, TPB)
    if key not in _CACHE:
        _CACHE[key] = build_program(NB, TPB)[0]
    nc = _CACHE[key]
    maps = make_in_maps(inputs, N_CORES, NB, seq)
    res = run_bass_kernel_spmd(nc, maps, core_ids=list(range(N_CORES)))
    out = np.concatenate([np.asarray(r["out"]).reshape(NB, seq, D) for r in res.results], axis=0)
    return out.astype(np.float32)
```

```python
import numpy as np
from contextlib import ExitStack
import concourse.bass as bass
import concourse.mybir as mybir
from concourse.bass_utils import run_bass_kernel_spmd

F32 = mybir.dt.float32
BF16 = mybir.dt.bfloat16
AF = mybir.ActivationFunctionType
ALU = mybir.AluOpType

P = 128
D = 1024
KC = 8
T = 512
NST = 4
FF = 2816
FC = 22
MEM = 256
CW = 31
HALO = 30
RMS_EPS = 1e-6
LN_EPS = 1e-5
N_CORES = 8
N_DENSE_TILES = 16
N_EASE2_TILES = 16

C_NMIX, C_BG0, C_BG1, C_CONVB, C_CLNG, C_CLNB, C_SLNG, C_NXAT, C_NMEM, C_NFFN, C_NFIN, C_CONVW = (
    0, 8, 16, 24, 32, 40, 48, 56, 64, 72, 80, 88)
NCOL = 88 + CW * 8


class _Op:
    __slots__ = ("eng", "fn", "deps", "dma_sem", "sig", "sem", "val", "semname", "final")

    def __init__(self, eng, fn, deps, dma_sem, final):
        self.eng = eng
        self.fn = fn
        self.deps = deps
        self.dma_sem = dma_sem
        self.final = final
        self.sig = False
        self.sem = None
        self.val = 0
        self.semname = None


class Sched:
    ENGS = ("pe", "act", "dve", "pool", "sp")

    def __init__(self):
        self.ops = []
        self.last_writer = {}
        self.readers = {}

    def add(self, eng, fn, reads=(), writes=(), dma_sem=None, sem_final=False):
        idx = len(self.ops)
        deps = set()
        for k in reads:
            w = self.last_writer.get(k)
            if w is not None:
                deps.add(w)
        for k in writes:
            w = self.last_writer.get(k)
            if w is not None:
                deps.add(w)
            rd = self.readers.get(k)
            if rd:
                deps.update(rd.values())
        deps.discard(idx)
        rkey = eng if dma_sem is None else ("dma", idx)
        for k in reads:
            self.readers.setdefault(k, {})[rkey] = idx
        for k in writes:
            self.last_writer[k] = idx
            self.readers[k] = {}
        self.ops.append(_Op(eng, fn, sorted(deps), dma_sem, sem_final))
        return idx

    @staticmethod
    def _pe_pe(dop, op):
        return dop.dma_sem is None and op.dma_sem is None and dop.eng == "pe" and op.eng == "pe"

    def emit(self, nc, stack):
        ops = self.ops
        for op in ops:
            for d in op.deps:
                if not self._pe_pe(ops[d], op):
                    ops[d].sig = True
        sems = {}
        counts = {}

        def get_sem(name):
            if name not in sems:
                sems[name] = stack.enter_context(nc.semaphore(name))
            return sems[name]

        for op in ops:
            if op.dma_sem is not None:
                name = "d_" + op.dma_sem
                counts[name] = counts.get(name, 0) + 16
            elif op.sig:
                name = "e_" + op.eng
                counts[name] = counts.get(name, 0) + 1
            else:
                continue
            op.sem = get_sem(name)
            op.val = counts[name]
            op.semname = name
        for op in ops:
            if op.final:
                op.val = counts[op.semname]
        self.counts = counts
        block = stack.enter_context(nc.Block())
        known = {e: {} for e in self.ENGS}

        def run_engine(ename, eobj):
            kn = known[ename]
            for op in ops:
                if op.eng != ename:
                    continue
                waits = {}
                for d in op.deps:
                    dop = ops[d]
                    if dop.sem is None or self._pe_pe(dop, op):
                        continue
                    nm = dop.semname
                    if kn.get(nm, 0) >= dop.val:
                        continue
                    if waits.get(nm, (None, 0))[1] < dop.val:
                        waits[nm] = (dop.sem, dop.val)
                wl = list(waits.values())
                for nm, (s, v) in waits.items():
                    kn[nm] = v
                if op.fn is None:
                    for s, v in wl:
                        eobj.wait_ge(s, v)
                    continue
                attach = None
                if wl and op.dma_sem is None:
                    attach = wl.pop()
                for s, v in wl:
                    eobj.wait_ge(s, v)
                ins = op.fn(eobj)
                if attach is not None:
                    ins._wait_ge(attach[0], attach[1])
                if op.dma_sem is not None:
                    ins.then_inc(op.sem, 16)
                elif op.sig:
                    ins.then_inc(op.sem, 1)

        @block.tensor
        def _(e):
            run_engine("pe", e)

        @block.scalar
        def _(e):
            run_engine("act", e)

        @block.vector
        def _(e):
            run_engine("dve", e)

        @block.gpsimd
        def _(e):
            run_engine("pool", e)

        @block.sync
        def _(e):
            run_engine("sp", e)


def build_program(NB, TPB, NS=4):
    NT = NB * TPB
    nc = bass.Bass("TRN2", target_bir_lowering=False)
    S = Sched()

    def din(name, shape, dt=F32):
        return nc.dram_tensor(name, list(shape), dt, kind="ExternalInput").ap()

    x_d = din("x", [NT * T, D])
    mem_d = din("mem", [NB * MEM, D])
    w_in_d = din("w_in", [D, 6 * D])
    w_co_d = din("w_co", [D, D])
    w_so_d = din("w_so", [D, D])
    w_mo_d = din("w_mo", [D, D])
    w_q_d = din("w_q", [D, D])
    w_kv_d = din("w_kv", [D, 2 * D])
    w_xo_d = din("w_xo", [D, D])
    w_gu_d = din("w_gu", [D, 2 * FF])
    w_dn_d = din("w_dn", [FF, D])
    pcols_d = din("pcols", [P, NCOL])
    ident_d = din("ident", [P, P])
    sguwT_d = din("sguwT", [P, 8 * P])
    brep_d = din("brep", [P, D])
    sgub_d = din("sgub", [1, D])
    out_d = nc.dram_tensor("out", [NT * T, D], F32, kind="ExternalOutput").ap()

    def dscr(name, npieces, nk=KC):
        return nc.dram_tensor(name, [npieces, P, nk, 512], BF16, kind="Internal").ap()

    s_in = dscr("s_in", 12)
    s_co = dscr("s_co", 2)
    s_so = dscr("s_so", 2)
    s_mo = dscr("s_mo", 2)
    s_q = dscr("s_q", 2)
    s_kv = dscr("s_kv", 4)
    s_xo = dscr("s_xo", 2)
    s_gu = dscr("s_gu", 11)
    s_dn = dscr("s_dn", 2, FC)

    st = ExitStack()
    with st:
        def sb(name, shape, dt):
            return st.enter_context(nc.sbuf_tensor("sb_" + name, list(shape), dt))

        pcols = sb("pcols", [P, NCOL], F32)
        identf = sb("identf", [P, P], F32)
        identb = sb("identb", [P, P], BF16)
        onesb = sb("onesb", [P, P], BF16)
        onesf = sb("onesf", [1, P], F32)
        mhalf = sb("mhalf", [P, 1], F32)
        wTm = sb("wTm", [P, 8, P], BF16)
        Cg = sb("Cg", [P, 8, P], F32)
        hT2 = sb("hT2", [P, KC, T], BF16)
        cv2 = sb("cv2", [P, KC, T], BF16)
        xin = sb("xin", [P, NST, D], F32)
        xout = sb("xout", [P, 2, D], F32)
        xTb = sb("xTb", [P, 2, KC, T], F32)
        hT = sb("hT", [P, KC, T], BF16)
        aT = sb("aT", [P, KC, T + HALO], BF16)
        uT = sb("uT", [P, KC, T], BF16)
        vt = sb("vt", [P, NST, D], BF16)
        gv = sb("gv", [P, 2, D], F32)
        ga = sb("ga", [P, KC, T], BF16)
        gb = sb("gb", [P, KC, T], BF16)
        cv = sb("cv", [P, KC, T], BF16)
        sg = sb("sg", [P, KC, T], BF16)
        sqb = sb("sqb", [P, 2, T], BF16)
        sgt = sb("sgt", [P, 2, T], BF16)
        ftmp = sb("ftmp", [P, 2, T], F32)
        nmb = sb("nmb", [P, T], F32)
        stat6 = sb("stat6", [P, 2, 2, 6], F32)
        mv = sb("mv", [P, 2, 2], F32)
        smal = sb("smal", [P, 2, 4], F32)
        KT = sb("KT", [P, KC, MEM], BF16)
        Vm = sb("Vm", [P, 2, D], BF16)
        memT = sb("memT", [P, KC, MEM], BF16)
        wring = sb("wring", [P, NS, KC, 512], BF16)
        pb = [st.enter_context(nc.psum_tensor(f"pb{i}", [P, 512], F32)) for i in range(8)]

        wTf = xout[:, 0, :].rearrange("p (g t) -> p g t", g=8)
        brep = xout[:, 1, :]
        memTf = xout[:].rearrange("p s (c m) -> p (s c) m", m=MEM)
        XOUT_KEYS = [("xout", 0), ("xout", 1)]
        sgub = gv[0:1, 1, :]

        cnt = {"mm": 0, "st": 0, "sq": 0, "sgt": 0, "ft": 0, "ws": 0, "dg": 0, "gv": 0, "xo": 0}

        def bank():
            i = cnt["mm"] % 4
            cnt["mm"] += 1
            return i

        def sbank():
            i = 6 + cnt["st"] % 2
            cnt["st"] += 1
            return i

        def rot(name, n):
            i = cnt[name] % n
            cnt[name] += 1
            return i

        def MM(bi, out_ap, lhsT, rhs, start, stop, reads):
            S.add("pe", lambda e: e.matmul(out_ap, lhsT=lhsT, rhs=rhs, start=start, stop=stop),
                  reads=reads, writes=[("pb", bi)])

        def ACT(out, in_, func, reads, writes, bias=None, scale=None):
            kw = {}
            if bias is not None:
                kw["bias"] = bias
            if scale is not None:
                kw["scale"] = scale
            S.add("act", lambda e: e.activation(out=out, in_=in_, func=func, **kw), reads=reads, writes=writes)

        def TT(eng, out, in0, in1, op, reads, writes):
            S.add(eng, lambda e: e.tensor_tensor(out=out, in0=in0, in1=in1, op=op), reads=reads, writes=writes)

        def STT(out, in0, scalar, in1, op0, op1, reads, writes):
            S.add("dve", lambda e: e.scalar_tensor_tensor(out=out, in0=in0, scalar=scalar, in1=in1, op0=op0, op1=op1),
                  reads=reads, writes=writes)

        def TS(eng, out, in0, s1, s2, op0, op1, reads, writes):
            if op1 is None:
                S.add(eng, lambda e: e.tensor_scalar(out=out, in0=in0, scalar1=s1, scalar2=None, op0=op0),
                      reads=reads, writes=writes)
            else:
                S.add(eng, lambda e: e.tensor_scalar(out=out, in0=in0, scalar1=s1, scalar2=s2, op0=op0, op1=op1),
                      reads=reads, writes=writes)

        def COPY(eng, out, in_, reads, writes):
            if eng == "act":
                ACT(out, in_, AF.Copy, reads, writes)
            else:
                S.add(eng, lambda e: e.tensor_copy(out=out, in_=in_), reads=reads, writes=writes)

        def col(off, c):
            return pcols[:, off + c:off + c + 1]

        def kcview(ap):
            return ap.rearrange("(kc p) c -> p kc c", p=P)

        prep = {}

        def prep_add(group, out_ap, in_ap):
            prep.setdefault(group, []).append((out_ap, in_ap))

        for j in range(4):
            prep_add("inA", s_in[j][:, :, 0:256], kcview(w_in_d[:, j * 256:(j + 1) * 256]))
            prep_add("inA", s_in[j][:, :, 256:512], kcview(w_in_d[:, D + j * 256:D + (j + 1) * 256]))
        for n in (6, 7):
            prep_add("inB", s_in[n], kcview(w_in_d[:, n * 512:(n + 1) * 512]))
        for n in (4, 5):
            prep_add("inC", s_in[n], kcview(w_in_d[:, n * 512:(n + 1) * 512]))
        for n in (8, 9, 10, 11):
            prep_add("inD", s_in[n], kcview(w_in_d[:, n * 512:(n + 1) * 512]))
        for nm, sc, wd, npc in (("co", s_co, w_co_d, 2), ("so", s_so, w_so_d, 2), ("mo", s_mo, w_mo_d, 2),
                                ("q", s_q, w_q_d, 2), ("kv", s_kv, w_kv_d, 4), ("xo", s_xo, w_xo_d, 2)):
            for n in range(npc):
                prep_add(nm, sc[n], kcview(wd[:, n * 512:(n + 1) * 512]))
        for j in range(11):
            grp = "gu%d" % (j // 4)
            prep_add(grp, s_gu[j][:, :, 0:256], kcview(w_gu_d[:, j * 256:(j + 1) * 256]))
            prep_add(grp, s_gu[j][:, :, 256:512], kcview(w_gu_d[:, FF + j * 256:FF + (j + 1) * 256]))
        for n in range(2):
            prep_add("dn%d" % n, s_dn[n], kcview(w_dn_d[:, n * 512:(n + 1) * 512]))
        prep_done = set()

        def ensure_prep(group):
            if group in prep_done:
                return
            prep_done.add(group)
            for i, (o, a) in enumerate(prep[group]):
                S.add("pool", lambda e, o=o, a=a: e.dma_start(out=o, in_=a),
                      writes=[("scr", group, i)], dma_sem="scr_" + group, sem_final=True)

        def scr_keys(group):
            return [("scr", group, i) for i in range(len(prep[group]))]

        def wload(src_ap, nk, group):
            ensure_prep(group)
            s = rot("ws", NS)
            S.add("sp", lambda e: e.dma_start(out=wring[:, s, 0:nk, :], in_=src_ap),
                  reads=scr_keys(group), writes=[("ws", s)], dma_sem="ws%d" % s)
            return s

        for dst, src, key in ((pcols[:], pcols_d[:, :], "pcols"), (identf[:], ident_d[:, :], "identf"),
                              (sgub, sgub_d[:, :], ("gv", 1))):
            S.add("sp", lambda e, dst=dst, src=src: e.dma_start(out=dst, in_=src), writes=[key],
                  dma_sem="setup", sem_final=True)
        S.add("sp", lambda e: e.dma_start(out=xout[:, 0, :], in_=sguwT_d[:, :]), writes=[("xout", 0)],
              dma_sem="setup", sem_final=True)
        S.add("sp", lambda e: e.dma_start(out=xout[:, 1, :], in_=brep_d[:, :]), writes=[("xout", 1)],
              dma_sem="setup", sem_final=True)

        def xload(ti):
            for s4 in range(NST):
                r0 = (ti * NST + s4) * P
                S.add("sp", lambda e, s4=s4, r0=r0: e.dma_start(out=xin[:, s4, :], in_=x_d[r0:r0 + P, :]),
                      writes=[("xin", s4)], dma_sem="xin%d" % s4)

        xload(0)

        COPY("dve", identb[:], identf[:], ["identf"], ["identb"])
        S.add("dve", lambda e: e.memset(onesb[:], 1.0), writes=["onesb"])
        S.add("dve", lambda e: e.memset(onesf[:], 1.0), writes=["onesf"])
        S.add("dve", lambda e: e.memset(mhalf[:], -0.5), writes=["mhalf"])
        S.add("pool", lambda e: e.affine_select(out=wTm[:], in_=wTf, pattern=[[0, 8], [1, P]], compare_op=ALU.is_ge,
                                                fill=0.0, base=0, channel_multiplier=-1),
              reads=[("xout", 0)], writes=["wTm"])
        S.add("pool", lambda e: e.affine_select(out=wTf, in_=wTf, pattern=[[0, 8], [1, P]], compare_op=ALU.is_ge,
                                                fill=0.0, base=0, channel_multiplier=-1),
              reads=[("xout", 0)], writes=[("xout", 0)])
        ensure_prep("inA")
        ensure_prep("inB")
        ensure_prep("kv")
        ensure_prep("inC")
        ensure_prep("inD")
        for g in range(8):
            bi = bank()
            o = pb[bi][:, 0:P]
            MM(bi, o, brep[:, g * P:(g + 1) * P], wTf[:, g, :], True, False, [("xout", 0), ("xout", 1)])
            MM(bi, o, onesf[0:1, :], sgub[0:1, g * P:(g + 1) * P], False, True, ["onesf", ("gv", 1)])
            COPY("dve", Cg[:, g, :], o, [("pb", bi)], ["Cg"])

        pend = []

        def defer(fn):
            pend.append([fn, 0])

        def tick(flush=False):
            keep = []
            for it in pend:
                it[1] += 1
                if flush or it[1] >= 2:
                    it[0]()
                else:
                    keep.append(it)
            pend[:] = keep

        def sq_accum(sbi, src_ap, src_key, N, first, last, deferred=True):
            i = rot("sq", 2)
            dst = sqb[:, i, 0:N]
            ACT(dst, src_ap, AF.Square, [src_key], [("sqb", i)])

            def f():
                MM(sbi, pb[sbi][:, 0:N], onesb[:], dst, first, last, ["onesb", ("sqb", i)])
            if deferred:
                defer(f)
            else:
                f()

        def rstd_from_ss(sbi, N, eps):
            i = rot("ft", 2)
            tmp = ftmp[:, i, 0:N]
            ACT(tmp, pb[sbi][:, 0:N], AF.Ln, [("pb", sbi)], [("ft", i)], bias=eps, scale=1.0 / D)
            ACT(pb[sbi][:, 0:N], tmp, AF.Exp, [("ft", i)], [("pb", sbi)], scale=-0.5)

        def norm_apply(sbi, N, src, dst, goff):
            for c in range(KC):
                STT(dst[c][0], src[c][0], col(goff, c), pb[sbi][:, 0:N], ALU.mult, ALU.mult,
                    [src[c][1], ("pb", sbi), "pcols"], [dst[c][1]])

        hT_c = [(hT[:, c, :], ("hT", c)) for c in range(KC)]

        def proj_fm(slot, nblk, rhs_list, nk, epilogue, k0=0):
            for m in range(nblk):
                bi = bank()
                for kk in range(nk):
                    MM(bi, pb[bi][:], wring[:, slot, kk, m * P:(m + 1) * P], rhs_list[k0 + kk][0],
                       kk == 0, kk == nk - 1, [("ws", slot), rhs_list[k0 + kk][1]])
                epilogue(m, bi)
                tick()

        def mem_part1(b):
            sbi = sbank()
            for mc in range(2):
                r0 = b * MEM + mc * P
                S.add("sp", lambda e, r0=r0: e.dma_start(out=gv[:, 0, :], in_=mem_d[r0:r0 + P, :]),
                      writes=[("gv", 0)], dma_sem="memld")
                for q in range(2):
                    bi = bank()
                    for c4 in range(4):
                        c = q * 4 + c4
                        S.add("pe", lambda e, bi=bi, c4=c4, c=c: e.transpose(
                            out=pb[bi][:, c4 * P:(c4 + 1) * P], in_=gv[:, 0, c * P:(c + 1) * P], identity=identf[:]),
                            reads=[("gv", 0), "identf"], writes=[("pb", bi)])
                    COPY("dve", memTf[:, q * 4:(q + 1) * 4, mc * P:(mc + 1) * P],
                         pb[bi][:].rearrange("p (c m) -> p c m", c=4), [("pb", bi)], XOUT_KEYS)
            for c in range(KC):
                sq_accum(sbi, memTf[:, c, :], ("xout", c // 4), MEM, c == 0, c == KC - 1, deferred=False)
            return sbi

        def mem_part2(sbi):
            rstd_from_ss(sbi, MEM, RMS_EPS)
            src = [(memTf[:, c, :], ("xout", c // 4)) for c in range(KC)]
            dst = [(memT[:, c, :], ("memT", c)) for c in range(KC)]
            norm_apply(sbi, MEM, src, dst, C_NMEM)
            for n in range(2):
                slot = wload(s_kv[n], KC, "kv")
                for m in range(4):
                    c = n * 4 + m
                    bi = bank()
                    for kk in range(KC):
                        MM(bi, pb[bi][:, 0:MEM], wring[:, slot, kk, m * P:(m + 1) * P], memT[:, kk, :],
                           kk == 0, kk == KC - 1, [("ws", slot), ("memT", kk)])
                    COPY("act", KT[:, c, :], pb[bi][:, 0:MEM], [("pb", bi)], [("KT", c)])
            for n in range(2):
                slot = wload(s_kv[2 + n], KC, "kv")
                for mc in range(2):
                    bi = bank()
                    for kk in range(KC):
                        MM(bi, pb[bi][:], memT[:, kk, mc * P:(mc + 1) * P], wring[:, slot, kk, :],
                           kk == 0, kk == KC - 1, [("ws", slot), ("memT", kk)])
                    COPY("act", Vm[:, mc, n * 512:(n + 1) * 512], pb[bi][:], [("pb", bi)], [("Vm", mc)])

        GA = [(ga[:, c, :], ("ga", c)) for c in range(KC)]
        GB = [(gb[:, c, :], ("gb", c)) for c in range(KC)]
        CV = [(cv[:, c, :], ("cv", c)) for c in range(KC)]
        SG = [(sg[:, c, :], ("sg", c)) for c in range(KC)]
        UT = [(uT[:, c, :], ("uT", c)) for c in range(KC)]
        HID = (GA + GB + CV)[:FC]
        vt_e = vt[:].rearrange("p s (h t) -> p (s h) t", t=T)
        ET = [(vt_e[:, i, :], ("vt", i // 2)) for i in range(8)]
        XT = [[(xTb[:, par, c, :], ("xT", par, c)) for c in range(KC)] for par in range(2)]
        front_sb = {}

        def residual_add_and_stats(xc, c, bi, sbi):
            TT("dve", xc[c][0], pb[bi][:], xc[c][0], ALU.add, [("pb", bi), xc[c][1]], [xc[c][1]])
            sq_accum(sbi, xc[c][0], xc[c][1], T, c == 0, c == KC - 1)

        def front_T(ti):
            par = ti % 2
            for s4 in range(NST):
                for q in range(2):
                    bi = bank()
                    for c4 in range(4):
                        c = q * 4 + c4
                        S.add("pe", lambda e, bi=bi, c4=c4, c=c, s4=s4: e.transpose(
                            out=pb[bi][:, c4 * P:(c4 + 1) * P], in_=xin[:, s4, c * P:(c + 1) * P], identity=identf[:]),
                            reads=[("xin", s4), "identf"], writes=[("pb", bi)])
                    COPY("act" if (q == 0 or ti < N_EASE2_TILES) else "dve", xTb[:, par, q * 4:(q + 1) * 4, s4 * P:(s4 + 1) * P],
                         pb[bi][:].rearrange("p (c m) -> p c m", c=4), [("pb", bi)],
                         [("xT", par, q * 4 + i) for i in range(4)])

        def front_ss(ti):
            par = ti % 2
            sbi = sbank()
            front_sb[ti] = sbi
            for c in range(KC):
                sq_accum(sbi, XT[par][c][0], XT[par][c][1], T, c == 0, c == KC - 1, deferred=False)

        def front_norm(ti):
            par = ti % 2
            sbi = front_sb[ti]
            rstd_from_ss(sbi, T, RMS_EPS)
            norm_apply(sbi, T, XT[par], hT_c, C_NMIX)

        def back_T_groups(ti):
            par = ti % 2
            groups = []
            for s4 in range(NST):
                box = {}
                for q in range(2):
                    def g(s4=s4, q=q, box=box):
                        if q == 0:
                            box["xs"] = rot("xo", 2)
                        xs = box["xs"]
                        bi = bank()
                        for c4 in range(4):
                            c = q * 4 + c4
                            S.add("pe", lambda e, bi=bi, c4=c4, c=c: e.transpose(
                                out=pb[bi][:, c4 * P:(c4 + 1) * P], in_=xTb[:, par, c, s4 * P:(s4 + 1) * P],
                                identity=identf[:]),
                                reads=[("xT", par, c), "identf"], writes=[("pb", bi)])
                        COPY("act" if (q == 0 or ti < N_EASE2_TILES) else "dve", xout[:, xs, q * 512:(q + 1) * 512],
                             pb[bi][:], [("pb", bi)], [("xout", xs)])
                        if q == 1:
                            r0 = (ti * NST + s4) * P
                            S.add("pool", lambda e: e.dma_start(out=out_d[r0:r0 + P, :], in_=xout[:, xs, :]),
                                  reads=[("xout", xs)], writes=[("outd", xs)], dma_sem="xo%d" % xs)
                    groups.append(g)
            return groups

        def back_T(ti):
            for g in back_T_groups(ti):
                g()

        def front_T_groups(ti):
            par = ti % 2
            groups = []
            for s4 in range(NST):
                for q in range(2):
                    def g(s4=s4, q=q):
                        bi = bank()
                        for c4 in range(4):
                            c = q * 4 + c4
                            S.add("pe", lambda e, bi=bi, c4=c4, c=c: e.transpose(
                                out=pb[bi][:, c4 * P:(c4 + 1) * P], in_=xin[:, s4, c * P:(c + 1) * P], identity=identf[:]),
                                reads=[("xin", s4), "identf"], writes=[("pb", bi)])
                        COPY("act" if (q == 0 or ti < N_EASE2_TILES) else "dve",
                             xTb[:, par, q * 4:(q + 1) * 4, s4 * P:(s4 + 1) * P],
                             pb[bi][:].rearrange("p (c m) -> p c m", c=4), [("pb", bi)],
                             [("xT", par, q * 4 + i) for i in range(4)])
                    groups.append(g)
            return groups

        HT2 = [(hT2[:, c, :], ("hT2", c)) for c in range(KC)]
        CV2 = [(cv2[:, c, :], ("cv2", c)) for c in range(KC)]
        convq = []

        def pump(n=None):
            k = len(convq) if n is None else min(n, len(convq))
            for _ in range(k):
                convq.pop(0)()

        def front_ss_chunk(ti, c):
            par = ti % 2
            if ti not in front_sb:
                front_sb[ti] = sbank()
            sq_accum(front_sb[ti], XT[par][c][0], XT[par][c][1], T, c == 0, c == KC - 1, deferred=True)

        def front_norm2(ti):
            par = ti % 2
            sbi = front_sb[ti]
            rstd_from_ss(sbi, T, RMS_EPS)
            norm_apply(sbi, T, XT[par], HT2, C_NMIX)

        def pre_a(ti):
            pos = ti % TPB
            if pos == 0:
                S.add("pool", lambda e: e.memset(aT[:, :, 0:HALO], 0.0), writes=[("aT", c) for c in range(KC)])
            else:
                S.add("pool", lambda e: e.tensor_copy(out=aT[:, :, 0:HALO], in_=aT[:, :, T:T + HALO]),
                      reads=[("aT", c) for c in range(KC)], writes=[("aT", c) for c in range(KC)])
            for j in range(4):
                slot = wload(s_in[j], KC, "inA")
                for i2 in range(2):
                    c = 2 * j + i2
                    bg = bank()
                    for kk in range(KC):
                        MM(bg, pb[bg][:], wring[:, slot, kk, (2 + i2) * P:(3 + i2) * P], hT2[:, kk, :],
                           kk == 0, kk == KC - 1, [("ws", slot), ("hT2", kk)])
                    si = rot("sgt", 2)
                    ACT(sgt[:, si, :], pb[bg][:], AF.Sigmoid, [("pb", bg)], [("sgt", si)])
                    bv = bank()
                    for kk in range(KC):
                        MM(bv, pb[bv][:], wring[:, slot, kk, i2 * P:(i2 + 1) * P], hT2[:, kk, :],
                           kk == 0, kk == KC - 1, [("ws", slot), ("hT2", kk)])
                    TT("dve", aT[:, c, HALO:HALO + T], pb[bv][:], sgt[:, si, :], ALU.mult,
                       [("pb", bv), ("sgt", si)], [("aT", c)])
            if ti == 0:
                for c in range(KC):
                    bi = bank()
                    for k in range(CW):
                        d = rot("dg", 16)
                        dap = sg[:, d // 4, (d % 4) * P:(d % 4 + 1) * P]
                        S.add("dve", lambda e, dap=dap, k=k, c=c: e.tensor_scalar(
                            out=dap, in0=identb[:], scalar1=col(C_CONVW, k * 8 + c), scalar2=None, op0=ALU.mult),
                            reads=["identb", "pcols"], writes=[("sgd", d)])
                        MM(bi, pb[bi][:], dap, aT[:, c, k:k + T], k == 0, k == CW - 1, [("sgd", d), ("aT", c)])
                    ACT(cv2[:, c, :], pb[bi][:], AF.Identity, [("pb", bi), "pcols"], [("cv2", c)], bias=col(C_CONVB, c))
                return
            for cp in range(0, KC, 2):
                for k in range(CW):
                    for i2 in range(2):
                        c = cp + i2
                        cb = 4 + i2

                        def f(c=c, cb=cb, k=k):
                            if k == 0:
                                TS("dve", pb[cb][:], aT[:, c, 0:T], col(C_CONVW, c), None, ALU.mult, None,
                                   [("aT", c), "pcols"], [("pb", cb)])
                            else:
                                STT(pb[cb][:], aT[:, c, k:k + T], col(C_CONVW, k * 8 + c), pb[cb][:], ALU.mult, ALU.add,
                                    [("aT", c), "pcols", ("pb", cb)], [("pb", cb)])
                        convq.append(f)
                for i2 in range(2):
                    c = cp + i2
                    cb = 4 + i2

                    def g(c=c, cb=cb):
                        ACT(cv2[:, c, :], pb[cb][:], AF.Identity, [("pb", cb), "pcols"], [("cv2", c)], bias=col(C_CONVB, c))
                    convq.append(g)

        front_T(0)
        front_ss(0)
        front_norm2(0)
        pre_a(0)

        for ti in range(NT):
            b, pos = divmod(ti, TPB)
            par = ti % 2
            xc = XT[par]
            dense_tile = ti < N_DENSE_TILES
            if ti + 1 < NT:
                xload(ti + 1)
            if pos == 0:
                mem_sbi = mem_part1(b)

            slotv = [wload(s_in[6], KC, "inB"), wload(s_in[7], KC, "inB")]
            for s4 in range(NST):
                gi = rot("gv", 2)
                for half in range(2):
                    bi = bank()
                    for kk in range(KC):
                        MM(bi, pb[bi][:], hT2[:, kk, s4 * P:(s4 + 1) * P], wring[:, slotv[half], kk, :],
                           kk == 0, kk == KC - 1, [("ws", slotv[half]), ("hT2", kk)])
                    ACT(gv[:, gi, half * 512:(half + 1) * 512], pb[bi][:], AF.Gelu_apprx_tanh,
                        [("pb", bi)], [("gv", gi)])
                for half in range(2):
                    S.add("dve", lambda e, gi=gi, half=half: e.bn_stats(
                        out=stat6[:, gi, half, :], in_=gv[:, gi, half * 512:(half + 1) * 512]),
                        reads=[("gv", gi)], writes=[("stat6", gi)])
                S.add("dve", lambda e, gi=gi: e.bn_aggr(out=mv[:, gi, :], in_=stat6[:, gi].rearrange("p a b -> p (a b)")),
                      reads=[("stat6", gi)], writes=[("mv", gi)])
                TS("dve", smal[:, gi, 0:1], mv[:, gi, 1:2], LN_EPS, None, ALU.add, None, [("mv", gi)], [("smal", gi)])
                TS("dve", smal[:, gi, 2:3], mv[:, gi, 0:1], -1.0, None, ALU.mult, None, [("mv", gi)], [("smal", gi)])
                TT("pool", smal[:, gi, 1:2], smal[:, gi, 0:1], mhalf[:], ALU.pow, [("smal", gi), "mhalf"], [("smal", gi)])
                TS("pool", vt[:, s4, :], gv[:, gi, :], smal[:, gi, 2:3], smal[:, gi, 1:2], ALU.add, ALU.mult,
                   [("gv", gi), ("smal", gi)], [("vt", s4)])
                pump(5)
            if pos == 0:
                mem_part2(mem_sbi)
            btq = back_T_groups(ti - 1) if ti > 0 else []
            for n in range(2):
                slot = wload(s_in[4 + n], KC, "inC")

                def ep(m, bi, n=n):
                    c = n * 4 + m
                    ACT(uT[:, c, :], pb[bi][:], AF.Gelu_apprx_tanh, [("pb", bi)], [("uT", c)])
                    pump(3)
                    if btq:
                        btq.pop(0)()
                proj_fm(slot, 4, HT2, KC, ep)
            while btq:
                btq.pop(0)()
            def sgu_group(g):
                bi = bank()
                for s4 in range(NST):
                    MM(bi, pb[bi][:, s4 * P:(s4 + 1) * P], vt[:, s4, g * P:(g + 1) * P], wTm[:, g, :], True, True,
                       [("vt", s4), "wTm"])
                v3 = pb[bi][:].rearrange("p (s t) -> p s t", s=NST)
                STT(v3, v3, col(C_SLNG, g), Cg[:, g, :].unsqueeze(1).broadcast_to([P, NST, P]), ALU.mult, ALU.add,
                    [("pb", bi), "pcols", "Cg"], [("pb", bi)])
                TT("dve", sg[:, g, :], pb[bi][:], uT[:, g, :], ALU.mult, [("pb", bi), ("uT", g)], [("sg", g)])

            for n in range(4):
                slot = wload(s_in[8 + n], KC, "inD")

                def ep(m, bi, n=n):
                    c = (n % 2) * 4 + m
                    dst, off = (GA, C_BG0) if n < 2 else (GB, C_BG1)
                    ACT(dst[c][0], pb[bi][:], AF.Sigmoid, [("pb", bi), "pcols"], [dst[c][1]], bias=col(off, c))
                    pump(1)
                    gidx = n * 4 + m
                    if dense_tile and gidx % 2 == 1:
                        sgu_group(gidx // 2)
                proj_fm(slot, 4, HT2, KC, ep)

            if ti == 0:
                for g_ in ("co", "so", "mo", "q", "xo"):
                    ensure_prep(g_)
            pump()
            if not dense_tile:
                for g in range(8):
                    sgu_group(g)

            s1b = sbank()
            s2b = sbank()
            for g in range(8):
                c = g
                i = rot("sq", 2)
                ACT(sqb[:, i, :], cv2[:, c, :], AF.Square, [("cv2", c)], [("sqb", i)])

                def fst(c=c, i=i):
                    MM(s1b, pb[s1b][:], onesb[:], cv2[:, c, :], c == 0, c == KC - 1, ["onesb", ("cv2", c)])
                    MM(s2b, pb[s2b][:], onesb[:], sqb[:, i, :], c == 0, c == KC - 1, ["onesb", ("sqb", i)])
                if g >= 1:
                    prev_fst()
                prev_fst = fst
            prev_fst()
            i_m2 = rot("ft", 2)
            ACT(ftmp[:, i_m2, :], pb[s1b][:], AF.Square, [("pb", s1b)], [("ft", i_m2)], scale=1.0 / D)
            STT(ftmp[:, i_m2, :], pb[s2b][:], 1.0 / D, ftmp[:, i_m2, :], ALU.mult, ALU.subtract,
                [("pb", s2b), ("ft", i_m2)], [("ft", i_m2)])
            ACT(ftmp[:, i_m2, :], ftmp[:, i_m2, :], AF.Ln, [("ft", i_m2)], [("ft", i_m2)], bias=LN_EPS)
            ACT(pb[s2b][:], ftmp[:, i_m2, :], AF.Exp, [("ft", i_m2)], [("pb", s2b)], scale=-0.5)
            ACT(nmb[:], pb[s1b][:], AF.Copy, [("pb", s1b)], ["nmb"], scale=-1.0 / D)

            ftq = front_T_groups(ti + 1) if ti + 1 < NT else []
            for n in range(2):
                slot = wload(s_so[n], KC, "so")

                def ep(m, bi, n=n):
                    c = n * 4 + m
                    TT("dve", gb[:, c, :], pb[bi][:], gb[:, c, :], ALU.mult, [("pb", bi), ("gb", c)], [("gb", c)])
                    if ftq:
                        ftq.pop(0)()
                proj_fm(slot, 4, SG, KC, ep)
                if n == 0:
                    for c in range(KC):
                        i = rot("ft", 2)
                        TT("pool", ftmp[:, i, :], cv2[:, c, :], nmb[:], ALU.add,
                           [("cv2", c), "nmb"], [("ft", i)])
                        TT("dve", ftmp[:, i, :], ftmp[:, i, :], pb[s2b][:], ALU.mult,
                           [("ft", i), ("pb", s2b)], [("ft", i)])
                        ACT(hT[:, c, :], ftmp[:, i, :], AF.Silu, [("ft", i), "pcols"], [("hT", c)],
                            bias=col(C_CLNB, c), scale=col(C_CLNG, c))
                    if ti == 0:
                        ensure_prep("gu0")
            while ftq:
                ftq.pop(0)()
            for n in range(2):
                slot = wload(s_co[n], KC, "co")

                def ep(m, bi, n=n):
                    c = n * 4 + m
                    TT("dve", ga[:, c, :], pb[bi][:], ga[:, c, :], ALU.mult, [("pb", bi), ("ga", c)], [("ga", c)])
                    TT("pool", uT[:, c, :], ga[:, c, :], gb[:, c, :], ALU.add, [("ga", c), ("gb", c)], [("uT", c)])
                    if ti + 1 < NT:
                        front_ss_chunk(ti + 1, c)
                proj_fm(slot, 4, hT_c, KC, ep)
            tick(flush=True)
            if ti + 1 < NT:
                front_norm2(ti + 1)

            if ti == 0:
                for g_ in ("gu0", "gu1"):
                    ensure_prep(g_)

            sbi = sbank()
            for n in range(2):
                slot = wload(s_mo[n], KC, "mo")

                def ep(m, bi, n=n):
                    residual_add_and_stats(xc, n * 4 + m, bi, sbi)
                proj_fm(slot, 4, UT, KC, ep)
            tick(flush=True)
            if ti == 0:
                ensure_prep("gu2")

            rstd_from_ss(sbi, T, RMS_EPS)
            norm_apply(sbi, T, xc, hT_c, C_NXAT)
            if ti + 1 < NT:
                pre_a(ti + 1)
            for n in range(2):
                slot = wload(s_q[n], KC, "q")

                def ep(m, bi, n=n):
                    c = n * 4 + m
                    ACT(cv[:, c, :], pb[bi][:], AF.Copy, [("pb", bi)], [("cv", c)], scale=1.0 / 16.0)
                    pump(2)
                proj_fm(slot, 4, hT_c, KC, ep)
            for h in range(4):
                for mc in range(2):
                    bi = bank()
                    for dc in range(2):
                        MM(bi, pb[bi][:], KT[:, 2 * h + dc, mc * P:(mc + 1) * P], cv[:, 2 * h + dc, :], dc == 0, dc == 1,
                           [("KT", 2 * h + dc), ("cv", 2 * h + dc)])
                    e_ap, e_key = ET[2 * h + mc]
                    ACT(e_ap, pb[bi][:], AF.Exp, [("pb", bi)], [e_key])
                bd = bank()
                for mc in range(2):
                    MM(bd, pb[bd][:], onesb[:], ET[2 * h + mc][0], mc == 0, mc == 1, ["onesb", ET[2 * h + mc][1]])
                i = rot("ft", 2)
                ACT(ftmp[:, i, :], pb[bd][:], AF.Ln, [("pb", bd)], [("ft", i)])
                ACT(ftmp[:, i, :], ftmp[:, i, :], AF.Exp, [("ft", i)], [("ft", i)], scale=-1.0)
                for dc in range(2):
                    c = 2 * h + dc
                    bo = bank()
                    for mc in range(2):
                        MM(bo, pb[bo][:], Vm[:, mc, c * P:(c + 1) * P], ET[2 * h + mc][0], mc == 0, mc == 1,
                           [("Vm", mc), ET[2 * h + mc][1]])
                    TT("dve", sg[:, c, :], pb[bo][:], ftmp[:, i, :], ALU.mult, [("pb", bo), ("ft", i)], [("sg", c)])
                pump(4)
            sbi = sbank()
            for n in range(2):
                slot = wload(s_xo[n], KC, "xo")

                def ep(m, bi, n=n):
                    residual_add_and_stats(xc, n * 4 + m, bi, sbi)
                    pump(2)
                proj_fm(slot, 4, SG, KC, ep)
            tick(flush=True)

            rstd_from_ss(sbi, T, RMS_EPS)
            norm_apply(sbi, T, xc, hT_c, C_NFFN)

            if ti == 0:
                for g_ in ("dn0", "dn1"):
                    ensure_prep(g_)
            for j in range(11):
                slot = wload(s_gu[j], KC, "gu%d" % (j // 4))
                for i2 in range(2):
                    fc = 2 * j + i2
                    bg = bank()
                    for kk in range(KC):
                        MM(bg, pb[bg][:], wring[:, slot, kk, i2 * P:(i2 + 1) * P], hT[:, kk, :],
                           kk == 0, kk == KC - 1, [("ws", slot), ("hT", kk)])
                    si = rot("sgt", 2)
                    ACT(sgt[:, si, :], pb[bg][:], AF.Silu, [("pb", bg)], [("sgt", si)])
                    bu = bank()
                    for kk in range(KC):
                        MM(bu, pb[bu][:], wring[:, slot, kk, (2 + i2) * P:(3 + i2) * P], hT[:, kk, :],
                           kk == 0, kk == KC - 1, [("ws", slot), ("hT", kk)])
                    TT("dve", HID[fc][0], pb[bu][:], sgt[:, si, :], ALU.mult, [("pb", bu), ("sgt", si)], [HID[fc][1]])
                    pump(5)
            sbi = sbank()
            for n in range(2):
                bks = [bank() for _ in range(4)]
                for kg, (k0, k1) in enumerate(((0, 8), (8, 16), (16, 22))):
                    slot = wload(s_dn[n][:, k0:k1, :], k1 - k0, "dn%d" % n)
                    for m in range(4):
                        for kk in range(k1 - k0):
                            MM(bks[m], pb[bks[m]][:], wring[:, slot, kk, m * P:(m + 1) * P], HID[k0 + kk][0],
                               k0 + kk == 0, k0 + kk == FC - 1, [("ws", slot), HID[k0 + kk][1]])
                        pump(2 if kg < 2 else (1 if m < 2 else 0))
                for m in range(4):
                    residual_add_and_stats(xc, n * 4 + m, bks[m], sbi)
                    tick()
            tick(flush=True)

            rstd_from_ss(sbi, T, RMS_EPS)
            norm_apply(sbi, T, xc, xc, C_NFIN)


        back_T(NT - 1)

        S.add("pool", None, reads=[("outd", 0), ("outd", 1)])
        S.emit(nc, st)
    return nc, S


def _cols(v):
    return np.ascontiguousarray(np.asarray(v, np.float32).reshape(-1, P).T)


def make_in_maps(inputs, n_cores, NB, seq):
    f = lambda k: np.asarray(inputs[k], np.float32)
    x = f("x")
    mem = f("mem")
    pc = np.zeros((P, NCOL), np.float32)
    bg = f("b_gate")[0]
    for off, v in ((C_NMIX, f("norm_mix")[0]), (C_BG0, bg[0]), (C_BG1, bg[1]), (C_CONVB, f("conv_b")[0]),
                   (C_CLNG, f("conv_ln_g")[0]), (C_CLNB, f("conv_ln_b")[0]), (C_SLNG, f("sgu_ln_g")[0]),
                   (C_NXAT, f("norm_xattn")[0]), (C_NMEM, f("norm_mem")[0]), (C_NFFN, f("norm_ffn")[0]),
                   (C_NFIN, f("norm_final"))):
        pc[:, off:off + 8] = _cols(v)
    cw = f("conv_w")[0]
    for k in range(CW):
        pc[:, C_CONVW + k * 8:C_CONVW + (k + 1) * 8] = _cols(cw[k])
    sguw = f("sgu_w")[0]
    sguwT = np.ascontiguousarray(sguw.transpose(2, 0, 1)).reshape(P, 8 * P)
    brep = np.ascontiguousarray(np.broadcast_to(f("sgu_ln_b")[0][None, :], (P, D)))
    sgub = np.ascontiguousarray(f("sgu_b")[0].reshape(1, D))
    common = {
        "w_in": np.ascontiguousarray(f("w_in")[0]), "w_co": np.ascontiguousarray(f("w_conv_out")[0]),
        "w_so": np.ascontiguousarray(f("w_sgu_out")[0]), "w_mo": np.ascontiguousarray(f("w_mix_out")[0]),
        "w_q": np.ascontiguousarray(f("w_q")[0]), "w_kv": np.ascontiguousarray(f("w_kv")[0]),
        "w_xo": np.ascontiguousarray(f("w_xo")[0]), "w_gu": np.ascontiguousarray(f("w_gu")[0]),
        "w_dn": np.ascontiguousarray(f("w_down")[0]),
        "pcols": pc, "ident": np.eye(P, dtype=np.float32), "sguwT": sguwT, "brep": brep, "sgub": sgub,
    }
    maps = []
    for ci in range(n_cores):
        m = dict(common)
        m["x"] = np.ascontiguousarray(x[ci * NB:(ci + 1) * NB].reshape(NB * seq, D))
        m["mem"] = np.ascontiguousarray(mem[ci * NB:(ci + 1) * NB].reshape(NB * MEM, D))
        maps.append(m)
    return maps


_CACHE = {}


def kernel(**inputs):
    x = np.asarray(inputs["x"])
    B, seq, _ = x.shape
    NB = B // N_CORES
    TPB = seq // T
    key = (NB, TPB)
    if key not in _CACHE:
        _CACHE[key] = build_program(NB, TPB)[0]
    nc = _CACHE[key]
    maps = make_in_maps(inputs, N_CORES, NB, seq)
    res = run_bass_kernel_spmd(nc, maps, core_ids=list(range(N_CORES)))
    out = np.concatenate([np.asarray(r["out"]).reshape(NB, seq, D) for r in res.results], axis=0)
    return out.astype(np.float32)
```
